# Optimizing a Trainium2 kernel written in Bass

```python
import math
import jax, jax.numpy as jnp
from jax import lax
import numpy as np

D_MODEL = 1024
BATCH = 8
SEQ = 8192
DEPTH = 1

HEAD_DIM = 64
NSA_HEADS = D_MODEL // 2 // HEAD_DIM
NSA_KV_GROUPS = 2
CMP_BLOCK = 32
CMP_STRIDE = 16
CMP_HIDDEN = 256
SEL_BLOCK = 64
SEL_TOPN = 16
NSA_WINDOW = 512
SWA_HEADS = D_MODEL // 2 // HEAD_DIM
SWA_KV_HEADS = 2
SWA_WINDOW = 128
REL_BUCKETS = 32
REL_MAX_DIST = 128
MEM_LEN = 256
XATTN_HEADS = 4
XATTN_HEAD_DIM = D_MODEL // XATTN_HEADS
D_FF = ((8 * D_MODEL // 3 + 127) // 128) * 128
NSA_WIDTH = NSA_HEADS * HEAD_DIM
NSA_KV_WIDTH = NSA_KV_GROUPS * HEAD_DIM
SWA_WIDTH = SWA_HEADS * HEAD_DIM
SWA_KV_WIDTH = SWA_KV_HEADS * HEAD_DIM
IN_WIDTH = NSA_WIDTH + 6 * NSA_KV_WIDTH + 3 * NSA_HEADS + SWA_WIDTH + 2 * SWA_KV_WIDTH + 2 * D_MODEL
Q_BLOCK = 128
EPS = 1e-6
NEG_INF = -1e30
FORCE = 1e4

kernel_name = 'hybrid_nsa_swa_sink_macaron_layer'


def rms_norm(x, g):
    xf = x.astype(jnp.float32)
    y = xf * lax.rsqrt(jnp.mean(xf * xf, axis=-1, keepdims=True) + EPS)
    return (y * g.astype(jnp.float32)).astype(x.dtype)


def swiglu(h, w_gate, w_up, w_down):
    return (jax.nn.silu(h @ w_gate) * (h @ w_up)) @ w_down


def rel_bucket(dist):
    dist = jnp.maximum(dist, 0)
    max_exact = REL_BUCKETS // 2
    d = jnp.maximum(dist, 1).astype(jnp.float32)
    large = max_exact + (jnp.log(d / max_exact) / math.log(REL_MAX_DIST / max_exact)
                         * (REL_BUCKETS - max_exact)).astype(jnp.int32)
    large = jnp.minimum(large, REL_BUCKETS - 1)
    return jnp.where(dist < max_exact, dist, large)


def masked_softmax(logits, mask):
    logits = jnp.where(mask, logits.astype(jnp.float32), NEG_INF)
    p = jax.nn.softmax(logits, axis=-1)
    return jnp.where(mask, p, 0.0)


def sink_softmax(logits, mask, sinks):
    logits = jnp.where(mask, logits.astype(jnp.float32), NEG_INF)
    sink = jnp.broadcast_to(sinks.astype(jnp.float32), logits.shape[:-1] + (1,))
    p = jax.nn.softmax(jnp.concatenate([logits, sink], axis=-1), axis=-1)[..., :-1]
    return jnp.where(mask, p, 0.0)


def token_mixing(h, w_in, cmp_pe_k, cmp_w1_k, cmp_w2_k, cmp_pe_v, cmp_w1_v, cmp_w2_v,
                 attn_sinks, rel_bias, w_up_a, w_up_b, w_out):
    B, S, _ = h.shape
    G, HPG = NSA_KV_GROUPS, NSA_HEADS // NSA_KV_GROUPS
    KB, HPB = SWA_KV_HEADS, SWA_HEADS // SWA_KV_HEADS
    sizes = [NSA_WIDTH] + [NSA_KV_WIDTH] * 6 + [3 * NSA_HEADS, SWA_WIDTH, SWA_KV_WIDTH, SWA_KV_WIDTH,
                                               D_MODEL, D_MODEL]
    splits = [int(s) for s in np.cumsum(sizes)[:-1]]
    proj = h @ w_in
    (q_a, k_c, v_c, k_s, v_s, k_w, v_w, g_nsa, q_b, k_b, v_b, gate_a, gate_b) = jnp.split(proj, splits, axis=-1)
    scale = HEAD_DIM ** -0.5
    q_a = q_a.reshape(B, S, G, HPG, HEAD_DIM) * scale
    k_c, v_c, k_s, v_s, k_w, v_w = [a.reshape(B, S, G, HEAD_DIM) for a in (k_c, v_c, k_s, v_s, k_w, v_w)]
    g_nsa = jax.nn.sigmoid(g_nsa).reshape(B, S, 3, G, HPG)
    q_b = q_b.reshape(B, S, KB, HPB, HEAD_DIM) * scale
    k_b = k_b.reshape(B, S, KB, HEAD_DIM)
    v_b = v_b.reshape(B, S, KB, HEAD_DIM)

    n_cmp = (S - CMP_BLOCK) // CMP_STRIDE + 1
    cmp_start = jnp.arange(n_cmp) * CMP_STRIDE
    cmp_end = cmp_start + CMP_BLOCK - 1
    tok = cmp_start[:, None] + jnp.arange(CMP_BLOCK)[None, :]

    def compress(a, pe, w1, w2):
        blocks = a[:, tok] + pe[None, None, :, None, :]
        flat = blocks.transpose(0, 1, 3, 2, 4).reshape(B, n_cmp, G, CMP_BLOCK * HEAD_DIM)
        return jax.nn.gelu(flat @ w1) @ w2

    kc = compress(k_c, cmp_pe_k, cmp_w1_k, cmp_w2_k)
    vc = compress(v_c, cmp_pe_v, cmp_w1_v, cmp_w2_v)

    n_sel = S // SEL_BLOCK
    top_n = min(SEL_TOPN, n_sel)
    sel_start = jnp.arange(n_sel) * SEL_BLOCK
    ov = (jnp.minimum(cmp_start[:, None] + CMP_BLOCK, sel_start[None, :] + SEL_BLOCK)
          - jnp.maximum(cmp_start[:, None], sel_start[None, :]))
    overlap = (jnp.maximum(ov, 0) / CMP_BLOCK).astype(jnp.float32)
    ks_blk = k_s.reshape(B, n_sel, SEL_BLOCK, G, HEAD_DIM)
    vs_blk = v_s.reshape(B, n_sel, SEL_BLOCK, G, HEAD_DIM)

    k_w_pad = jnp.pad(k_w, ((0, 0), (NSA_WINDOW, 0), (0, 0), (0, 0)))
    v_w_pad = jnp.pad(v_w, ((0, 0), (NSA_WINDOW, 0), (0, 0), (0, 0)))
    k_b_pad = jnp.pad(k_b, ((0, 0), (SWA_WINDOW, 0), (0, 0), (0, 0)))
    v_b_pad = jnp.pad(v_b, ((0, 0), (SWA_WINDOW, 0), (0, 0), (0, 0)))

    bias_a = rel_bias[:, :NSA_HEADS].reshape(REL_BUCKETS, G, HPG)
    bias_b = rel_bias[:, NSA_HEADS:].reshape(REL_BUCKETS, KB, HPB)
    sinks = attn_sinks.reshape(KB, HPB, 1)
    b_ix = jnp.arange(B)[:, None, None, None]
    g_ix = jnp.arange(G)[None, None, :, None]
    blk_j = jnp.arange(n_sel)

    def query_block(c):
        s0 = c * Q_BLOCK
        t = s0 + jnp.arange(Q_BLOCK)
        qa = lax.dynamic_slice_in_dim(q_a, s0, Q_BLOCK, axis=1)

        dist_c = t[:, None] - cmp_end[None, :]
        s_c = jnp.einsum('bqghd,bngd->bqghn', qa, kc) + bias_a[rel_bucket(dist_c)].transpose(0, 2, 3, 1)
        p_c = masked_softmax(s_c, (dist_c >= 0)[:, None, None, :])
        o_c = jnp.einsum('bqghn,bngd->bqghd', p_c.astype(vc.dtype), vc)

        imp = jnp.einsum('bqghn,nj->bqgj', p_c, overlap)
        cur = (t // SEL_BLOCK)[:, None]
        forced = (blk_j[None] == 0) | (blk_j[None] == cur) | (blk_j[None] == cur - 1)
        causal = blk_j[None] * SEL_BLOCK <= t[:, None]
        imp = jnp.where(forced[:, None, :], FORCE, jnp.where(causal[:, None, :], imp, -FORCE))
        _, idx = lax.top_k(imp, top_n)
        ks = ks_blk[b_ix, idx, :, g_ix, :].reshape(B, Q_BLOCK, G, top_n * SEL_BLOCK, HEAD_DIM)
        vs = vs_blk[b_ix, idx, :, g_ix, :].reshape(B, Q_BLOCK, G, top_n * SEL_BLOCK, HEAD_DIM)
        kpos = (idx[..., None] * SEL_BLOCK + jnp.arange(SEL_BLOCK)).reshape(B, Q_BLOCK, G, top_n * SEL_BLOCK)
        dist_s = t[None, :, None, None] - kpos
        s_s = (jnp.einsum('bqghd,bqgkd->bqghk', qa, ks)
               + jnp.moveaxis(bias_a[rel_bucket(dist_s), g_ix], -1, 3))
        p_s = masked_softmax(s_s, (dist_s >= 0)[:, :, :, None, :])
        o_s = jnp.einsum('bqghk,bqgkd->bqghd', p_s.astype(vs.dtype), vs)

        kw = lax.dynamic_slice_in_dim(k_w_pad, s0, NSA_WINDOW + Q_BLOCK, axis=1)
        vw = lax.dynamic_slice_in_dim(v_w_pad, s0, NSA_WINDOW + Q_BLOCK, axis=1)
        kwpos = s0 - NSA_WINDOW + jnp.arange(NSA_WINDOW + Q_BLOCK)
        dist_w = t[:, None] - kwpos[None, :]
        mask_w = (dist_w >= 0) & (dist_w < NSA_WINDOW) & (kwpos[None, :] >= 0)
        s_w = jnp.einsum('bqghd,bkgd->bqghk', qa, kw) + bias_a[rel_bucket(dist_w)].transpose(0, 2, 3, 1)
        p_w = masked_softmax(s_w, mask_w[:, None, None, :])
        o_w = jnp.einsum('bqghk,bkgd->bqghd', p_w.astype(vw.dtype), vw)

        g = lax.dynamic_slice_in_dim(g_nsa, s0, Q_BLOCK, axis=1)
        o_a = g[:, :, 0, :, :, None] * o_c + g[:, :, 1, :, :, None] * o_s + g[:, :, 2, :, :, None] * o_w

        qb = lax.dynamic_slice_in_dim(q_b, s0, Q_BLOCK, axis=1)
        kb = lax.dynamic_slice_in_dim(k_b_pad, s0, SWA_WINDOW + Q_BLOCK, axis=1)
        vb = lax.dynamic_slice_in_dim(v_b_pad, s0, SWA_WINDOW + Q_BLOCK, axis=1)
        kbpos = s0 - SWA_WINDOW + jnp.arange(SWA_WINDOW + Q_BLOCK)
        dist_b = t[:, None] - kbpos[None, :]
        mask_b = (dist_b >= 0) & (dist_b < SWA_WINDOW) & (kbpos[None, :] >= 0)
        s_b = jnp.einsum('bqnhd,bjnd->bqnhj', qb, kb) + bias_b[rel_bucket(dist_b)].transpose(0, 2, 3, 1)
        p_b = sink_softmax(s_b, mask_b[:, None, None, :], sinks)
        o_b = jnp.einsum('bqnhj,bjnd->bqnhd', p_b.astype(vb.dtype), vb)
        return o_a.reshape(B, Q_BLOCK, NSA_WIDTH), o_b.reshape(B, Q_BLOCK, SWA_WIDTH)

    o_a, o_b = lax.map(query_block, jnp.arange(S // Q_BLOCK))
    o_a = o_a.transpose(1, 0, 2, 3).reshape(B, S, NSA_WIDTH)
    o_b = o_b.transpose(1, 0, 2, 3).reshape(B, S, SWA_WIDTH)
    merged = jax.nn.sigmoid(gate_a) * (o_a @ w_up_a) + jax.nn.sigmoid(gate_b) * (o_b @ w_up_b)
    return merged @ w_out


def memory_xattn(h, m, w_xq, w_xkv, w_xo):
    B, S, _ = h.shape
    q = (h @ w_xq).reshape(B, S, XATTN_HEADS, XATTN_HEAD_DIM) * XATTN_HEAD_DIM ** -0.5
    k, v = jnp.split(m @ w_xkv, 2, axis=-1)
    k = k.reshape(B, -1, XATTN_HEADS, XATTN_HEAD_DIM)
    v = v.reshape(B, -1, XATTN_HEADS, XATTN_HEAD_DIM)
    p = jax.nn.softmax(jnp.einsum('bshd,bmhd->bhsm', q, k).astype(jnp.float32), axis=-1)
    o = jnp.einsum('bhsm,bmhd->bshd', p.astype(v.dtype), v)
    return o.reshape(B, S, D_MODEL) @ w_xo


def setup_inputs(seed: int = 0) -> dict:
    key = jax.random.key(seed)
    keys = iter(jax.random.split(key, 40))

    def w(shape, fan_in):
        return jax.random.normal(next(keys), shape, jnp.float32) * fan_in ** -0.5

    def gain(n=D_MODEL):
        return 1.0 + 0.01 * jax.random.normal(next(keys), (DEPTH, n), jnp.float32)

    L = DEPTH
    return {
        'x': jax.random.normal(next(keys), (BATCH, SEQ, D_MODEL), jnp.float32),
        'mem': jax.random.normal(next(keys), (BATCH, MEM_LEN, D_MODEL), jnp.float32),
        'norm_ffn1': gain(),
        'w1_gate': w((L, D_MODEL, D_FF), D_MODEL),
        'w1_up': w((L, D_MODEL, D_FF), D_MODEL),
        'w1_down': w((L, D_FF, D_MODEL), D_FF),
        'norm_mix': gain(),
        'w_in': w((L, D_MODEL, IN_WIDTH), D_MODEL),
        'cmp_pe_k': 0.1 * jax.random.normal(next(keys), (L, CMP_BLOCK, HEAD_DIM), jnp.float32),
        'cmp_w1_k': w((L, CMP_BLOCK * HEAD_DIM, CMP_HIDDEN), CMP_BLOCK * HEAD_DIM),
        'cmp_w2_k': w((L, CMP_HIDDEN, HEAD_DIM), CMP_HIDDEN),
        'cmp_pe_v': 0.1 * jax.random.normal(next(keys), (L, CMP_BLOCK, HEAD_DIM), jnp.float32),
        'cmp_w1_v': w((L, CMP_BLOCK * HEAD_DIM, CMP_HIDDEN), CMP_BLOCK * HEAD_DIM),
        'cmp_w2_v': w((L, CMP_HIDDEN, HEAD_DIM), CMP_HIDDEN),
        'attn_sinks': jax.random.normal(next(keys), (L, SWA_HEADS), jnp.float32),
        'rel_bias': 0.5 * jax.random.normal(next(keys), (REL_BUCKETS, NSA_HEADS + SWA_HEADS), jnp.float32),
        'w_up_a': w((L, NSA_WIDTH, D_MODEL), NSA_WIDTH),
        'w_up_b': w((L, SWA_WIDTH, D_MODEL), SWA_WIDTH),
        'w_out': w((L, D_MODEL, D_MODEL), D_MODEL),
        'norm_xattn': gain(),
        'norm_mem': gain(),
        'w_xq': w((L, D_MODEL, D_MODEL), D_MODEL),
        'w_xkv': w((L, D_MODEL, 2 * D_MODEL), D_MODEL),
        'w_xo': w((L, D_MODEL, D_MODEL), D_MODEL),
        'norm_ffn2': gain(),
        'w2_gate': w((L, D_MODEL, D_FF), D_MODEL),
        'w2_up': w((L, D_MODEL, D_FF), D_MODEL),
        'w2_down': w((L, D_FF, D_MODEL), D_FF),
        'norm_final': 1.0 + 0.01 * jax.random.normal(next(keys), (D_MODEL,), jnp.float32),
    }


def reference(x, mem, norm_ffn1, w1_gate, w1_up, w1_down, norm_mix, w_in,
              cmp_pe_k, cmp_w1_k, cmp_w2_k, cmp_pe_v, cmp_w1_v, cmp_w2_v,
              attn_sinks, rel_bias, w_up_a, w_up_b, w_out,
              norm_xattn, norm_mem, w_xq, w_xkv, w_xo,
              norm_ffn2, w2_gate, w2_up, w2_down, norm_final):
    for l in range(DEPTH):
        x = x + 0.5 * swiglu(rms_norm(x, norm_ffn1[l]), w1_gate[l], w1_up[l], w1_down[l])
        x = x + token_mixing(rms_norm(x, norm_mix[l]), w_in[l],
                             cmp_pe_k[l], cmp_w1_k[l], cmp_w2_k[l], cmp_pe_v[l], cmp_w1_v[l], cmp_w2_v[l],
                             attn_sinks[l], rel_bias, w_up_a[l], w_up_b[l], w_out[l])
        x = x + memory_xattn(rms_norm(x, norm_xattn[l]), rms_norm(mem, norm_mem[l]), w_xq[l], w_xkv[l], w_xo[l])
        x = x + 0.5 * swiglu(rms_norm(x, norm_ffn2[l]), w2_gate[l], w2_up[l], w2_down[l])
    return rms_norm(x, norm_final)
```

```python
import numpy as np
import ml_dtypes
from contextlib import ExitStack
import concourse.bass as bass
import concourse.mybir as mybir
from concourse.bass_utils import run_bass_kernel_spmd

F32 = mybir.dt.float32
BF16 = mybir.dt.bfloat16
ALU = mybir.AluOpType
AF = mybir.ActivationFunctionType
AX = mybir.AxisListType

S = 8192
D = 1024
DFF = 2816
NFF = DFF // 128
INW = 4120
NT = S // 128
EPS = 1e-6
NEG = -30000.0
MEM = 256

O_QA, O_KC, O_VC, O_KS, O_VS, O_KW, O_VW, O_G, O_QB, O_KB, O_VB, O_GA, O_GB = (
    0, 512, 640, 768, 896, 1024, 1152, 1280, 1304, 1816, 1944, 2072, 3096)


class Res:
    def __init__(self, t=None, name=""):
        self.t = t
        self.name = name
        self.w = None
        self.r = {}

    def __getitem__(self, k):
        return self.t[k]


class Eng:
    def __init__(self, h, sem, name):
        self.h = h
        self.sem = sem
        self.name = name
        self.count = 0
        self.seen = {}


class KB:
    def __init__(self, nc, es):
        self.nc = nc
        self.es = es
        mk = lambda n: es.enter_context(nc.semaphore(n))
        self.pe = Eng(nc.tensor, mk("s_pe"), "pe")
        self.act = Eng(nc.scalar, mk("s_act"), "act")
        self.dve = Eng(nc.vector, mk("s_dve"), "dve")
        self.pool = Eng(nc.gpsimd, mk("s_pool"), "pool")
        self.sp = Eng(nc.sync, mk("s_sp"), "sp")
        self.dsems = [[mk("d%d" % i), 0] for i in range(48)]
        self.dnext = 0
        self.dnext_sw = 0
        self.n_ins = 0
        self.trace = {}

    def _wait(self, eng, ev, raw):
        if ev is None:
            return
        key, sem, val = ev
        if key == eng.name:
            if eng.name in ("pe", "sp"):
                return
        if eng.seen.get(key, 0) >= val:
            return
        eng.h.wait_ge(sem, val)
        eng.seen[key] = val
        self.n_ins += 1
        self.trace.setdefault(eng.name, []).append(("w", key, val))

    def _deps(self, eng, reads, writes):
        for t in reads:
            self._wait(eng, t.w, True)
        for t in writes:
            self._wait(eng, t.w, False)
            for ev in t.r.values():
                self._wait(eng, ev, False)

    def _mark(self, ev, reads, writes):
        for t in writes:
            t.w = ev
            t.r = {}
        for t in reads:
            t.r[ev[0]] = ev

    def op(self, eng, fn, reads=(), writes=(), inc=True):
        self._deps(eng, reads, writes)
        ins = fn()
        self.n_ins += 1
        if inc:
            eng.count += 1
            ins.then_inc(eng.sem, 1)
            self.trace.setdefault(eng.name, []).append(("i", eng.name, 1))
            ev = (eng.name, eng.sem, eng.count)
        else:
            ev = (eng.name, eng.sem, eng.count + 1)
        self._mark(ev, reads, writes)
        return ins

    def dma(self, q, out, in_, reads=(), writes=(), **kw):
        self._deps(q, reads, writes)
        if q.name == "pool":
            i = 36 + self.dnext_sw
            self.dnext_sw = (self.dnext_sw + 1) % 12
        else:
            i = self.dnext
            self.dnext = (self.dnext + 1) % 36
        sem, tot = self.dsems[i]
        key = "d%d" % i
        if tot:
            self._wait(q, (key, sem, tot), True)
        tot += 16
        self.dsems[i][1] = tot
        q.h.dma_start(out=out, in_=in_, **kw).then_inc(sem, 16)
        self.n_ins += 1
        self.trace.setdefault(q.name, []).append(("i", key, 16))
        ev = (key, sem, tot)
        self._mark(ev, reads, writes)
        return ev

    def sb(self, es, name, shape, dt):
        self.uid = getattr(self, "uid", 0) + 1
        name = "%s_u%d" % (name, self.uid)
        return Res(es.enter_context(self.nc.sbuf_tensor(name, list(shape), dt)), name)

    def ps(self, es, name, shape, dt):
        return Res(es.enter_context(self.nc.psum_tensor(name, list(shape), dt)), name)

    def dram(self, name, shape, dt):
        return Res(self.nc.dram_tensor(name, list(shape), dt).ap(), name)

    def mm(self, out, lhsT, rhs, start, stop, reads, writes, inc=None, skip=False):
        if inc is None:
            inc = stop
        return self.op(self.pe, lambda: self.nc.tensor.matmul(out, lhsT, rhs, start=start, stop=stop, skip_group_check=skip),
                       reads, writes, inc)

    def tr(self, out, in_, ident, reads, writes, inc=True):
        return self.op(self.pe, lambda: self.nc.tensor.transpose(out, in_, ident), reads, writes, inc)

    def actv(self, out, in_, func, reads, writes, **kw):
        return self.op(self.act, lambda: self.nc.scalar.activation(out, in_, func, **kw), reads, writes)

    def barrier(self):
        engs = [self.pe, self.act, self.dve, self.pool, self.sp]
        for e in engs:
            for o in engs:
                if o is not e and o.count:
                    self._wait(e, (o.name, o.sem, o.count), True)
            for i, (sem, tot) in enumerate(self.dsems):
                if tot:
                    self._wait(e, ("d%d" % i, sem, tot), True)

    def check_deadlock(self):
        pos = {k: 0 for k in self.trace}
        sems = {}
        prog = True
        while prog:
            prog = False
            for k, tr in self.trace.items():
                while pos[k] < len(tr):
                    kind, key, val = tr[pos[k]]
                    if kind == "w":
                        if sems.get(key, 0) < val:
                            break
                    else:
                        sems[key] = sems.get(key, 0) + val
                    pos[k] += 1
                    prog = True
        bad = {k: (pos[k], len(tr), tr[pos[k]]) for k, tr in self.trace.items() if pos[k] < len(tr)}
        if bad:
            raise RuntimeError("DEADLOCK: %s ; sems=%s" % (bad, {k: v for k, v in sems.items() if not k.startswith("d")}))

    def finish(self):
        for i, (sem, tot) in enumerate(self.dsems):
            if tot:
                self._wait(self.sp, ("d%d" % i, sem, tot), True)


class Ctx:
    pass


def load_w(kb, q, dst, src, kc, ncols=None, col0=0, dcol0=0):
    for c in range(kc):
        n = ncols if ncols is not None else src.shape[1]
        kb.dma(q, dst.t[:, c, dcol0:dcol0 + n], src[c * 128:(c + 1) * 128, col0:col0 + n], writes=[dst])


def rstd_ops(kb, c, ss):
    kb.actv(ss.t[:, 1:2], ss.t[:, 0:1], AF.Ln, [ss, c.epsb], [ss], scale=1.0 / D, bias=c.epsb.t[:, 0:1])
    kb.actv(ss.t[:, 2:3], ss.t[:, 1:2], AF.Exp, [ss], [ss], scale=-0.5)


def rmsnorm_T(kb, c, x_ap, xres, g, hT, hslice, junk, ss, hbf, ptr):
    nc = kb.nc
    kb.actv(junk.t[:, :], x_ap, AF.Square, [xres], [junk, ss], accum_out=ss.t[:, 0:1])
    rstd_ops(kb, c, ss)
    kb.op(kb.dve, lambda: nc.vector.scalar_tensor_tensor(out=hbf.t[:, :], in0=x_ap, scalar=ss.t[:, 2:3], in1=g.t[:, :],
                                                         op0=ALU.mult, op1=ALU.mult), [xres, ss, g], [hbf])
    for k in range(8):
        kb.tr(ptr.t[:, k, :], hbf.t[:, k * 128:(k + 1) * 128], c.ident.t[:, :], [hbf, c.ident], [ptr], inc=(k == 7))
    kb.op(kb.act, lambda: nc.scalar.copy(out=hT.t[:, 0:8, hslice], in_=ptr.t[:, 0:8, :]), [ptr], [hT])


def load_gain(kb, q, dst, src1d):
    kb.dma(q, dst.t[:, :], src1d.partition_broadcast(128), writes=[dst])


def ffn_phase(kb, c, xin, xout, gnorm, wg_d, wu_d, wd_d, final_g=None, ntiles=None):
    nc = kb.nc
    TT = 256
    nsub = TT // 128
    ntt = S // TT if ntiles is None else ntiles
    kb.barrier()
    with ExitStack() as es:
        wg = kb.sb(es, "wg", [128, 8, DFF], BF16)
        wu = kb.sb(es, "wu", [128, 8, DFF], BF16)
        wd = kb.sb(es, "wd", [128, NFF, D], BF16)
        g = kb.sb(es, "g_ffn", [128, D], F32)
        gf = kb.sb(es, "g_fin", [128, D], F32) if final_g is not None else None
        xs = [kb.sb(es, "xs%d" % i, [128, nsub, D], F32) for i in range(2)]
        hT = [kb.sb(es, "hT%d" % i, [128, 8, TT], BF16) for i in range(2)]
        aT = kb.sb(es, "aT", [128, NFF, TT], BF16)
        sg = [kb.sb(es, "sg%d" % i, [128, TT], F32) for i in range(2)]
        junk = kb.sb(es, "junk", [128, D], F32)
        ss = [kb.sb(es, "ss%d" % i, [128, 4], F32) for i in range(2)]
        hbf = [kb.sb(es, "hbf%d" % i, [128, D], BF16) for i in range(2)]
        yo = [kb.sb(es, "yo%d" % i, [128, D], F32) for i in range(2)]
        load_gain(kb, kb.sp, g, gnorm)
        if gf is not None:
            load_gain(kb, kb.sp, gf, final_g)
        load_w(kb, kb.pool, wg, wg_d, 8)
        load_w(kb, kb.pool, wu, wu_d, 8)
        load_w(kb, kb.pool, wd, wd_d, NFF)
        pg = c.psb[0:2]
        pu = c.psb[2:4]
        py = c.psb[4:6]
        ptr = c.ptr
        xview = xin.t.rearrange("(n s p) d -> n p s d", p=128, s=nsub)
        oview = xout.t.rearrange("(n s p) d -> n p s d", p=128, s=nsub)

        def load(i):
            kb.dma(kb.sp, xs[i % 2].t[:, :, :], xview[i], reads=[xin], writes=[xs[i % 2]])
        load(0)
        for i in range(ntt):
            if i + 1 < ntt:
                load(i + 1)
            xt = xs[i % 2]
            h = hT[i % 2]
            for s in range(nsub):
                rmsnorm_T(kb, c, xt.t[:, s, :], xt, g, h, slice(s * 128, (s + 1) * 128), junk, ss[s % 2], hbf[s % 2], ptr)
            for j in range(NFF):
                pgj = pg[j % 2]
                puj = pu[j % 2]
                for k in range(8):
                    kb.mm(pgj.t[:, 0:TT], wg.t[:, k, j * 128:(j + 1) * 128], h.t[:, k, :], k == 0, k == 7, [wg, h], [pgj])
                for k in range(8):
                    kb.mm(puj.t[:, 0:TT], wu.t[:, k, j * 128:(j + 1) * 128], h.t[:, k, :], k == 0, k == 7, [wu, h], [puj])
                sgj = sg[j % 2]
                kb.actv(sgj.t[:, :], pgj.t[:, 0:TT], AF.Silu, [pgj], [sgj])
                kb.op(kb.dve, lambda: nc.vector.tensor_tensor(out=aT.t[:, j, :], in0=sgj.t[:, :], in1=puj.t[:, 0:TT], op=ALU.mult),
                      [sgj, puj], [aT])
            for s in range(nsub):
                for o in range(2):
                    pyo = py[o]
                    for j in range(NFF):
                        kb.mm(pyo.t[:, :], aT.t[:, j, s * 128:(s + 1) * 128], wd.t[:, j, o * 512:(o + 1) * 512],
                              j == 0, j == NFF - 1, [aT, wd], [pyo])
                y = yo[s % 2]
                for o in range(2):
                    kb.op(kb.dve, lambda: nc.vector.scalar_tensor_tensor(
                        out=y.t[:, o * 512:(o + 1) * 512], in0=py[o].t[:, :], scalar=0.5, in1=xt.t[:, s, o * 512:(o + 1) * 512],
                        op0=ALU.mult, op1=ALU.add), [py[o], xt], [y])
                if gf is not None:
                    sq = ss[s % 2]
                    kb.actv(junk.t[:, :], y.t[:, :], AF.Square, [y], [junk, sq], accum_out=sq.t[:, 0:1])
                    rstd_ops(kb, c, sq)
                    kb.op(kb.dve, lambda: nc.vector.scalar_tensor_tensor(out=y.t[:, :], in0=y.t[:, :], scalar=sq.t[:, 2:3],
                                                                         in1=gf.t[:, :], op0=ALU.mult, op1=ALU.mult),
                          [y, sq, gf], [y])
                kb.dma(kb.sp, oview[i][:, s, :], y.t[:, :], reads=[y], writes=[xout])


def proj_phase(kb, c, x1_d, inp, sc, ntiles=None):
    nc = kb.nc
    TT = 512
    ntt = S // TT if ntiles is None else ntiles
    w_in = inp["w_in"]
    kb.barrier()
    with ExitStack() as es:
        wi = kb.sb(es, "wi", [128, 8, 2072], BF16)
        g = kb.sb(es, "g_mix", [128, D], F32)
        xs = [kb.sb(es, "pxs%d" % i, [128, 4, D], F32) for i in range(2)]
        hT = kb.sb(es, "phT", [128, 8, TT], BF16)
        junk = kb.sb(es, "pjunk", [128, D], F32)
        ss = [kb.sb(es, "pss%d" % i, [128, 4], F32) for i in range(2)]
        hbf = [kb.sb(es, "phbf%d" % i, [128, D], BF16) for i in range(2)]
        qst = [kb.sb(es, "qst%d" % i, [128, 4, TT], BF16) for i in range(2)]
        kst = [kb.sb(es, "kst%d" % i, [128, TT], BF16) for i in range(5)]
        vst = [kb.sb(es, "vst%d" % i, [128, 4, 130], BF16) for i in range(3)]
        gst = kb.sb(es, "gst", [128, 4, 24], F32)
        load_gain(kb, kb.sp, g, inp["norm_mix"])
        for k in range(8):
            rows = slice(k * 128, (k + 1) * 128)
            for (o0, d0) in ((O_QA, 0), (O_QB, O_QB)):
                for hh in range(4):
                    kb.dma(kb.pool, wi.t[:, k, d0 + hh * 128:d0 + (hh + 1) * 128].rearrange("p (g d) -> p g d", g=2),
                           w_in[rows, o0:o0 + 512].rearrange("p (g h d) -> p h g d", g=2, h=4)[:, hh], writes=[wi])
            kb.dma(kb.pool, wi.t[:, k, 512:O_QB], w_in[rows, 512:O_QB], writes=[wi])
            kb.dma(kb.pool, wi.t[:, k, O_KB:2072], w_in[rows, O_KB:2072], writes=[wi])
        for v in vst:
            kb.op(kb.dve, lambda: nc.vector.memset(v.t[:, :, :], 1.0), [], [v])
        xview = x1_d.t.rearrange("(n s p) d -> n p s d", p=128, s=4)

        def load(i):
            kb.dma(kb.sp, xs[i % 2].t[:, :, :], xview[i], reads=[x1_d], writes=[xs[i % 2]])
        load(0)
        fm = [(O_QA + 128 * j, "qa", j) for j in range(4)] + [(O_QB + 128 * j, "qb", j) for j in range(4)] + \
             [(O_KC, "k", 0), (O_VC, "k", 1), (O_KS, "k", 2), (O_KW, "k", 3), (O_KB, "k", 4)]
        kdst = [sc.kcT_d, sc.vcT_d, sc.ksT_d, sc.kwT_d, sc.kbT_d]
        pi = 0
        for i in range(ntt):
            if i + 1 < ntt:
                load(i + 1)
            xt = xs[i % 2]
            for s in range(4):
                rmsnorm_T(kb, c, xt.t[:, s, :], xt, g, hT, slice(s * 128, (s + 1) * 128), junk, ss[s % 2], hbf[s % 2], c.ptr)
            for (col, kind, j) in fm:
                pp = c.psb[pi % 4]
                pi += 1
                for k in range(8):
                    kb.mm(pp.t[:, :], wi.t[:, k, col:col + 128], hT.t[:, k, :], k == 0, k == 7, [wi, hT], [pp])
                if kind == "qa" or kind == "qb":
                    st = qst[0 if kind == "qa" else 1]
                    kb.actv(st.t[:, j, :], pp.t[:, :], AF.Copy, [pp], [st], scale=0.125)
                    if j == 3:
                        dd = sc.qa_d if kind == "qa" else sc.qb_d
                        for nn in range(4):
                            kb.dma(kb.sp, dd.t[4 * i + nn], st.t[:, :, nn * 128:(nn + 1) * 128], reads=[st], writes=[dd])
                else:
                    st = kst[j]
                    kb.actv(st.t[:, :], pp.t[:, :], AF.Copy, [pp], [st])
                    kb.dma(kb.sp, kdst[j].t[:, i * TT:(i + 1) * TT], st.t[:, :], reads=[st], writes=[kdst[j]])
            for s in range(4):
                pp = c.psb[4 + s % 2]
                for (o0, n, p0) in ((O_VS, 128, 0), (O_VW, 152, 128), (O_VB, 128, 280)):
                    for k in range(8):
                        kb.mm(pp.t[:, p0:p0 + n], hT.t[:, k, s * 128:(s + 1) * 128], wi.t[:, k, o0:o0 + n], k == 0, k == 7,
                              [wi, hT], [pp])
                for vi, p0 in enumerate((0, 128, 280)):
                    kb.op(kb.dve, lambda: nc.vector.tensor_copy(
                        out=vst[vi].t[:, s, :].rearrange("p (g e) -> p g e", g=2)[:, :, 0:64],
                        in_=pp.t[:, p0:p0 + 128].rearrange("p (g e) -> p g e", g=2)), [pp], [vst[vi]])
                kb.actv(gst.t[:, s, :], pp.t[:, 256:280], AF.Sigmoid, [pp, vst[0], vst[1], vst[2]], [gst])
            for vi, dd in enumerate((sc.vs_d, sc.vw_d, sc.vb_d)):
                kb.dma(kb.sp, dd.t[i * TT:(i + 1) * TT, :].rearrange("(s p) e -> p s e", p=128), vst[vi].t[:, :, :],
                       reads=[vst[vi]], writes=[dd])
            kb.dma(kb.sp, sc.g_d.t[i * TT:(i + 1) * TT, :].rearrange("(s p) e -> p s e", p=128), gst.t[:, :, :],
                   reads=[gst], writes=[sc.g_d])


GELU_C = 1.5957691216057308


def cmp_phase(kb, c, inp, sc, at):
    nc = kb.nc
    kb.barrier()
    with ExitStack() as es:
        kcT = kb.sb(es, "kcT", [128, S], BF16)
        vcT = kb.sb(es, "vcT", [128, S], BF16)
        w1 = [kb.sb(es, "cw1%d" % i, [128, 32, 256], BF16) for i in range(2)]
        w2kp = kb.sb(es, "w2kp", [128, 2, 2, 128], BF16)
        w2v = kb.sb(es, "w2v", [128, 2, 64], BF16)
        peT = [kb.sb(es, "peT%d" % i, [128, 32], BF16) for i in range(2)]
        cb = kb.sb(es, "cb", [128, 4], F32)
        u = kb.sb(es, "cu", [128, 512], F32)
        t = kb.sb(es, "ct", [128, 512], F32)
        sg = kb.sb(es, "csg", [128, 512], F32)
        hid = [[kb.sb(es, "hid%d%d" % (w, g), [128, 2, 512], BF16) for g in range(2)] for w in range(2)]
        kb.dma(kb.sp, kcT.t[:, :], sc.kcT_d.t[:, :], reads=[sc.kcT_d], writes=[kcT])
        kb.dma(kb.sp, vcT.t[:, :], sc.vcT_d.t[:, :], reads=[sc.vcT_d], writes=[vcT])
        for w, (n1, npe) in enumerate((("cmp_w1_k", "cmp_pe_k"), ("cmp_w1_v", "cmp_pe_v"))):
            for dup in range(2):
                kb.dma(kb.pool, w1[w].t[dup * 64:(dup + 1) * 64, :, :], inp[n1].rearrange("(l d) h -> d l h", d=64), writes=[w1[w]])
                kb.dma(kb.pool, peT[w].t[dup * 64:(dup + 1) * 64, :], inp[npe].rearrange("l d -> d l"), writes=[peT[w]],
                       allow_slow_non_contiguous=True)
        kb.op(kb.dve, lambda: nc.vector.memset(w2kp.t[:, :, :, :], 0.0), [], [w2kp])
        for hc in range(2):
            for g in range(2):
                kb.dma(kb.pool, w2kp.t[:, hc, g, g * 64:(g + 1) * 64], inp["cmp_w2_k"][hc * 128:(hc + 1) * 128, :], writes=[w2kp])
            kb.dma(kb.pool, w2v.t[:, hc, :], inp["cmp_w2_v"][hc * 128:(hc + 1) * 128, :], writes=[w2v])
        for w in range(2):
            for g in range(2):
                kb.op(kb.dve, lambda: nc.vector.memset(hid[w][g].t[:, :, :], 0.0), [], [hid[w][g]])
        src = [kcT, vcT]
        pi = 0
        for w in range(2):
            for hc in range(2):
                pp = c.psb[pi % 4]; pi += 1
                for l in range(32):
                    kb.mm(pp.t[:, 0:1], w1[w].t[0:64, l, hc * 128:(hc + 1) * 128], peT[w].t[0:64, l:l + 1], l == 0, l == 31,
                          [w1[w], peT[w]], [pp])
                kb.op(kb.dve, lambda: nc.vector.tensor_copy(out=cb.t[:, w * 2 + hc:w * 2 + hc + 1], in_=pp.t[:, 0:1]), [pp], [cb])
            for g in range(2):
                gp = slice(g * 64, (g + 1) * 64)
                for hc in range(2):
                    pp = c.psb[pi % 4]; pi += 1
                    for l in range(32):
                        kb.mm(pp.t[:, 0:511], w1[w].t[gp, l, hc * 128:(hc + 1) * 128], src[w].t[gp, l:l + 16 * 510 + 1:16],
                              l == 0, l == 31, [w1[w], src[w]], [pp])
                    kb.actv(u.t[:, 0:511], pp.t[:, 0:511], AF.Identity, [pp, cb], [u], bias=cb.t[:, w * 2 + hc:w * 2 + hc + 1])
                    kb.op(kb.dve, lambda: nc.vector.tensor_tensor(out=t.t[:, 0:511], in0=u.t[:, 0:511], in1=u.t[:, 0:511], op=ALU.mult),
                          [u], [t])
                    kb.op(kb.dve, lambda: nc.vector.tensor_scalar(out=t.t[:, 0:511], in0=t.t[:, 0:511], scalar1=0.044715, scalar2=1.0,
                                                                  op0=ALU.mult, op1=ALU.add), [t], [t])
                    kb.op(kb.dve, lambda: nc.vector.tensor_tensor(out=t.t[:, 0:511], in0=t.t[:, 0:511], in1=u.t[:, 0:511], op=ALU.mult),
                          [t, u], [t])
                    kb.actv(sg.t[:, 0:511], t.t[:, 0:511], AF.Sigmoid, [t], [sg], scale=GELU_C)
                    kb.op(kb.dve, lambda: nc.vector.tensor_tensor(out=hid[w][g].t[:, hc, 0:511], in0=u.t[:, 0:511], in1=sg.t[:, 0:511],
                                                                  op=ALU.mult), [u, sg], [hid[w][g]])
        pp = c.psb[4]
        n = 0
        for g in range(2):
            for hc in range(2):
                kb.mm(pp.t[:, :], w2kp.t[:, hc, g, :], hid[0][g].t[:, hc, :], n == 0, n == 3, [w2kp, hid[0][g]], [pp])
                n += 1
        kb.op(kb.dve, lambda: nc.vector.tensor_copy(out=at.kcc.t[:, :], in_=pp.t[:, :]), [pp], [at.kcc])
        kb.op(kb.dve, lambda: nc.vector.memset(at.vcc.t[:, :, :], 1.0), [], [at.vcc])
        kb.op(kb.dve, lambda: nc.vector.memset(at.vcn.t[:, :, :], 1.0), [], [at.vcn])
        pp = c.psb[5]
        for m in range(4):
            for g in range(2):
                for hc in range(2):
                    kb.mm(pp.t[:, (m * 2 + g) * 64:(m * 2 + g + 1) * 64], hid[1][g].t[:, hc, m * 128:(m + 1) * 128], w2v.t[:, hc, :],
                          hc == 0, hc == 1, [hid[1][g], w2v], [pp])
        kb.op(kb.dve, lambda: nc.vector.tensor_copy(
            out=at.vcc.t[:, :, :].rearrange("p m (g e) -> p (m g) e", g=2)[:, :, 0:64],
            in_=pp.t[:, :].rearrange("p (a e) -> p a e", e=64)), [pp], [at.vcc])
        for c4 in range(16):
            pp = c.psb[c4 % 4]
            for cc in range(4):
                ct = c4 * 4 + cc
                n0 = max(0, 8 * ct - 9)
                for g in range(2):
                    for hc in range(2):
                        kb.mm(pp.t[0:16, (cc * 2 + g) * 64:(cc * 2 + g + 1) * 64], hid[1][g].t[:, hc, n0:n0 + 16], w2v.t[:, hc, :],
                              hc == 0, hc == 1, [hid[1][g], w2v], [pp])
            kb.op(kb.dve, lambda: nc.vector.tensor_copy(
                out=at.vcn.t[0:16, c4 * 4:(c4 + 1) * 4, :].rearrange("p m (g e) -> p (m g) e", g=2)[:, :, 0:64],
                in_=pp.t[0:16, :].rearrange("p (a e) -> p a e", e=64)), [pp], [at.vcn])

def attn_phase(kb, c, inp, sc, at, ntiles=None):
    nc = kb.nc
    ntq = NT if ntiles is None else ntiles
    kb.barrier()
    with ExitStack() as es:
        ksT = kb.sb(es, "ksT", [128, S], BF16)
        vs = kb.sb(es, "vs", [128, NT, 130], BF16)
        em = kb.sb(es, "em", [128, S], BF16)
        TW = kb.sb(es, "tw_sb", [128, 8, 640], F32)
        TB = kb.sb(es, "tb_sb", [128, 8, 256], F32)
        TC = kb.sb(es, "tc_sb", [16, 3, 8, 128], F32)
        addt = kb.sb(es, "addt_sb", [128, 256], F32)
        ovf = kb.sb(es, "ovf_sb", [128, 4, 128], BF16)
        ovn = kb.sb(es, "ovn_sb", [16, 256], BF16)
        b31 = kb.sb(es, "b31_sb", [128, 16], F32)
        snk = kb.sb(es, "snk", [128, 8], F32)
        qa = [kb.sb(es, "qa%d" % i, [128, 512], BF16) for i in range(2)]
        qb = [kb.sb(es, "qb%d" % i, [128, 512], BF16) for i in range(2)]
        kw = [kb.sb(es, "kw%d" % i, [128, 640], BF16) for i in range(2)]
        vw = [kb.sb(es, "vw%d" % i, [128, 5, 130], BF16) for i in range(2)]
        kbt = [kb.sb(es, "kbt%d" % i, [128, 256], BF16) for i in range(2)]
        vb = [kb.sb(es, "vb%d" % i, [128, 2, 130], BF16) for i in range(2)]
        gt = [kb.sb(es, "gt%d" % i, [128, 24], F32) for i in range(2)]
        NE = 4
        E = [kb.sb(es, "E%d" % i, [128, 512], BF16) for i in range(NE)]
        tmp = [kb.sb(es, "tmp%d" % i, [128, 512], F32) for i in range(2)]
        imp = kb.sb(es, "imp", [128, 128], F32)
        imp2 = kb.sb(es, "imp2", [128, 128], F32)
        m8 = kb.sb(es, "m8", [128, 16], F32)
        thr = kb.sb(es, "thr", [128, 1], F32)
        negm = kb.sb(es, "negm", [128, 128], BF16)
        negT4 = kb.sb(es, "negT4", [128, 512], BF16)
        rz = kb.sb(es, "rz", [128, 16], F32)
        fac = kb.sb(es, "fac", [128, 16], F32)
        acc = [kb.sb(es, "acc%d" % i, [128, 64], F32) for i in range(2)]
        oab = [kb.sb(es, "oab%d" % i, [128, 1024], BF16) for i in range(2)]

        kb.dma(kb.sp, ksT.t[:, :], sc.ksT_d.t[:, :], reads=[sc.ksT_d], writes=[ksT])
        for q4 in range(4):
            kb.dma(kb.sp, vs.t[:, q4 * 16:(q4 + 1) * 16, :],
                   sc.vs_d.t[q4 * 2048:(q4 + 1) * 2048, :].rearrange("(n p) e -> p n e", p=128), reads=[sc.vs_d], writes=[vs])
        kb.dma(kb.sp, em.t[:, :], inp["emaster"], writes=[em])
        kb.dma(kb.sp, TW.t[:, :, :], inp["tw"], writes=[TW])
        kb.dma(kb.sp, TB.t[:, :, :], inp["tb"], writes=[TB])
        kb.dma(kb.sp, TC.t[:, :, :, :], inp["tc"], writes=[TC])
        kb.dma(kb.sp, addt.t[:, :], inp["addt"], writes=[addt])
        kb.dma(kb.sp, ovf.t[:, :, :], inp["ovf"], writes=[ovf])
        kb.dma(kb.sp, ovn.t[:, :], inp["ovn"], writes=[ovn])
        kb.dma(kb.sp, b31.t[:, :], inp["b31"], writes=[b31])
        kb.dma(kb.sp, snk.t[:, :], inp["attn_sinks"].partition_broadcast(128), writes=[snk])
        kb.actv(snk.t[:, :], snk.t[:, :], AF.Exp, [snk], [snk])
        for h in range(8):
            kb.op(kb.dve, lambda: nc.vector.tensor_scalar(out=TW.t[:, h, :], in0=TW.t[:, h, :], scalar1=b31.t[:, h:h + 1], scalar2=None,
                                                          op0=ALU.subtract), [TW, b31], [TW])
            kb.op(kb.dve, lambda: nc.vector.tensor_scalar(out=TC.t[:, :, h, :], in0=TC.t[:, :, h, :], scalar1=b31.t[0:16, h:h + 1],
                                                          scalar2=None, op0=ALU.subtract), [TC, b31], [TC])

        pS = c.psb[0:2]
        pU, pOC, pOS, pOW, pOB = c.psb[2], c.psb[3], c.psb[4], c.psb[5], c.psb[6]
        st = {"ps": 0, "e": 0, "t": 0}

        def load(ct):
            b = ct % 2
            lo = max(0, ct - 4)
            nk = ct + 1 - lo
            kb.dma(kb.sp, qa[b].t[:, :].rearrange("p (h q) -> p h q", h=4), sc.qa_d.t[ct], reads=[sc.qa_d], writes=[qa[b]])
            kb.dma(kb.sp, qb[b].t[:, :].rearrange("p (h q) -> p h q", h=4), sc.qb_d.t[ct], reads=[sc.qb_d], writes=[qb[b]])
            kb.dma(kb.sp, kw[b].t[:, 0:nk * 128], sc.kwT_d.t[:, lo * 128:(ct + 1) * 128], reads=[sc.kwT_d], writes=[kw[b]])
            kb.dma(kb.sp, vw[b].t[:, 0:nk, :], sc.vw_d.t[lo * 128:(ct + 1) * 128, :].rearrange("(n p) e -> p n e", p=128),
                   reads=[sc.vw_d], writes=[vw[b]])
            lob = max(0, ct - 1)
            nb = ct + 1 - lob
            kb.dma(kb.sp, kbt[b].t[:, 0:nb * 128], sc.kbT_d.t[:, lob * 128:(ct + 1) * 128], reads=[sc.kbT_d], writes=[kbt[b]])
            kb.dma(kb.sp, vb[b].t[:, 0:nb, :], sc.vb_d.t[lob * 128:(ct + 1) * 128, :].rearrange("(n p) e -> p n e", p=128),
                   reads=[sc.vb_d], writes=[vb[b]])
            kb.dma(kb.sp, gt[b].t[:, :], sc.g_d.t[ct * 128:(ct + 1) * 128, :], reads=[sc.g_d], writes=[gt[b]])

        def run_items(items):
            pend = None
            for it in items + [None]:
                cur = None
                if it is not None:
                    rows = it["rows"]
                    pp = pS[st["ps"] % 2]; st["ps"] += 1
                    kb.mm(pp.t[0:rows, :], it["k"][0], it["q"][0], True, it["mask"] is None, it["k"][1] + it["q"][1], [pp])
                    if it["mask"] is not None:
                        kb.mm(pp.t[0:rows, :], it["mask"], negT4.t[:, :], False, True, [em, negT4], [pp])
                    e = E[st["e"] % NE]; st["e"] += 1
                    if it["bias"] is not None:
                        tt = tmp[st["t"] % 2]; st["t"] += 1
                        kb.op(kb.dve, lambda: nc.vector.tensor_tensor(
                            out=tt.t[0:rows, :].rearrange("p (h q) -> p h q", h=4),
                            in0=pp.t[0:rows, :].rearrange("p (h q) -> p h q", h=4), in1=it["bias"][0], op=ALU.add),
                            [pp] + it["bias"][1], [tt])
                        kb.actv(e.t[0:rows, :], tt.t[0:rows, :], AF.Exp, [tt], [e])
                    else:
                        kb.actv(e.t[0:rows, :], pp.t[0:rows, :], AF.Exp, [pp], [e])
                    cur = (it, e)
                if pend is not None:
                    pit, pe_ = pend
                    rows = pit["rows"]
                    for (o, r, rd, wr) in pit["pv"]:
                        for h in range(4):
                            kb.mm(o(h), pe_.t[0:rows, h * 128:(h + 1) * 128], r, pit["first"] and h == 0, pit["last"] and h == 3,
                                  [pe_] + rd, wr, inc=(pit["last"] and h == 3), skip=True)
                pend = cur

        load(0)
        for ct in range(ntq):
            if ct + 1 < ntq:
                load(ct + 1)
            b = ct % 2
            lo = max(0, ct - 4)
            lob = max(0, ct - 1)
            ob = oab[b]
            for g in range(2):
                gp = slice(g * 64, (g + 1) * 64)
                qag = (qa[b].t[gp, :], [qa[b]])
                qbg = (qb[b].t[gp, :], [qb[b]])
                items = []
                nfar = 8 * ct - 9
                m = 0
                while 128 * m < nfar:
                    rows = min(128, nfar - 128 * m)
                    items.append(dict(rows=rows, k=(at.kcc.t[gp, 128 * m:128 * m + rows], [at.kcc]), q=qag, mask=None, bias=None,
                                      pv=[(lambda h: pOC.t[:, h * 65:(h + 1) * 65], at.vcc.t[0:rows, m, g * 65:(g + 1) * 65], [at.vcc], [pOC]),
                                          (lambda h: pU.t[:, h * 128:(h + 1) * 128], ovf.t[0:rows, m, :], [ovf], [pU])]))
                    m += 1
                n0 = max(0, 8 * ct - 9)
                var = 0 if ct >= 2 else (1 if ct == 1 else 2)
                items.append(dict(rows=16, k=(at.kcc.t[gp, n0:n0 + 16], [at.kcc]), q=qag, mask=None,
                                  bias=(TC.t[:, var, 4 * g:4 * g + 4, :], [TC]),
                                  pv=[(lambda h: pOC.t[:, h * 65:(h + 1) * 65], at.vcn.t[0:16, ct, g * 65:(g + 1) * 65], [at.vcn], [pOC]),
                                      (lambda h: pU.t[:, h * 128:(h + 1) * 128], ovn.t[0:16, 128 - 2 * ct:256 - 2 * ct], [ovn], [pU])]))
                for i, it in enumerate(items):
                    it["first"] = (i == 0)
                    it["last"] = (i == len(items) - 1)
                run_items(items)
                kb.op(kb.dve, lambda: nc.vector.tensor_scalar(out=rz.t[:, 0:4], in0=pOC.t[:, 64:260:65], scalar1=1e-30, scalar2=None,
                                                              op0=ALU.max), [pOC], [rz])
                kb.op(kb.dve, lambda: nc.vector.reciprocal(out=rz.t[:, 0:4], in_=rz.t[:, 0:4]), [rz], [rz])
                kb.op(kb.dve, lambda: nc.vector.scalar_tensor_tensor(out=imp.t[:, :], in0=pU.t[:, 0:128], scalar=rz.t[:, 0:1],
                                                                     in1=addt.t[:, 128 - 2 * ct:256 - 2 * ct], op0=ALU.mult, op1=ALU.add),
                      [pU, rz, addt], [imp])
                for h in range(1, 4):
                    kb.op(kb.dve, lambda: nc.vector.scalar_tensor_tensor(out=imp.t[:, :], in0=pU.t[:, h * 128:(h + 1) * 128],
                                                                         scalar=rz.t[:, h:h + 1], in1=imp.t[:, :], op0=ALU.mult, op1=ALU.add),
                          [pU, rz, imp], [imp])
                kb.op(kb.dve, lambda: nc.vector.tensor_scalar(out=imp.t[:, 0:1], in0=imp.t[:, 0:1], scalar1=1.0e4, scalar2=None, op0=ALU.add),
                      [imp], [imp])
                kb.op(kb.dve, lambda: nc.vector.max(out=m8.t[:, 0:8], in_=imp.t[:, :]), [imp], [m8])
                kb.op(kb.dve, lambda: nc.vector.match_replace(out=imp2.t[:, :], in_to_replace=m8.t[:, 0:8], in_values=imp.t[:, :],
                                                              imm_value=-3.0e4), [imp, m8], [imp2])
                kb.op(kb.dve, lambda: nc.vector.max(out=m8.t[:, 8:16], in_=imp2.t[:, :]), [imp2], [m8])
                kb.op(kb.dve, lambda: nc.vector.tensor_reduce(out=thr.t[:, :], in_=m8.t[:, 8:16], axis=AX.X, op=ALU.min), [m8], [thr])
                kb.op(kb.dve, lambda: nc.vector.tensor_scalar(out=negm.t[:, :], in0=imp.t[:, :], scalar1=thr.t[:, 0:1], scalar2=NEG,
                                                              op0=ALU.is_lt, op1=ALU.mult), [imp, thr], [negm])
                items = []
                for kt in range(lo, ct + 1):
                    w = ct - kt
                    sl = kt - lo
                    items.append(dict(rows=128, k=(kw[b].t[gp, sl * 128:(sl + 1) * 128], [kw[b]]), q=qag, mask=None,
                                      bias=((TW.t[:, 4 * g:4 * g + 4, 128 * w:128 * w + 128], [TW]) if w in (0, 1, 4) else None),
                                      pv=[(lambda h: pOW.t[:, h * 65:(h + 1) * 65], vw[b].t[:, sl, g * 65:(g + 1) * 65], [vw[b]], [pOW])]))
                for i, it in enumerate(items):
                    it["first"] = (i == 0)
                    it["last"] = (i == len(items) - 1)
                run_items(items)
                items = []
                for kt in range(lob, ct + 1):
                    w = ct - kt
                    sl = kt - lob
                    items.append(dict(rows=128, k=(kbt[b].t[gp, sl * 128:(sl + 1) * 128], [kbt[b]]), q=qbg, mask=None,
                                      bias=(TB.t[:, 4 * g:4 * g + 4, 128 * w:128 * w + 128], [TB]),
                                      pv=[(lambda h: pOB.t[:, h * 65:(h + 1) * 65], vb[b].t[:, sl, g * 65:(g + 1) * 65], [vb[b]], [pOB])]))
                for i, it in enumerate(items):
                    it["first"] = (i == 0)
                    it["last"] = (i == len(items) - 1)
                run_items(items)
                kb.tr(c.ptr.t[:, 0, :], negm.t[:, :], c.ident.t[:, :], [negm, c.ident], [c.ptr])
                for h in range(4):
                    kb.op(kb.dve, lambda: nc.vector.tensor_copy(out=negT4.t[:, h * 128:(h + 1) * 128], in_=c.ptr.t[:, 0, :]), [c.ptr], [negT4])
                items = []
                for kt in range(0, ct + 1):
                    w = ct - kt
                    items.append(dict(rows=128, k=(ksT.t[gp, kt * 128:(kt + 1) * 128], [ksT]), q=qag,
                                      mask=(em.t[:, kt * 128:(kt + 1) * 128] if kt < ct else None),
                                      bias=((TW.t[:, 4 * g:4 * g + 4, 128 * w:128 * w + 128], [TW]) if w in (0, 1) else None),
                                      pv=[(lambda h: pOS.t[:, h * 65:(h + 1) * 65], vs.t[:, kt, g * 65:(g + 1) * 65], [vs], [pOS])]))
                for i, it in enumerate(items):
                    it["first"] = (i == 0)
                    it["last"] = (i == len(items) - 1)
                run_items(items)
                for bi, pO in enumerate((pOS, pOW)):
                    kb.op(kb.dve, lambda: nc.vector.reciprocal(out=rz.t[:, 4 + 4 * bi:8 + 4 * bi], in_=pO.t[:, 64:260:65]), [pO], [rz])
                kb.op(kb.dve, lambda: nc.vector.tensor_tensor(out=rz.t[:, 12:16], in0=pOB.t[:, 64:260:65], in1=snk.t[:, 4 * g:4 * g + 4],
                                                              op=ALU.add), [pOB, snk], [rz])
                kb.op(kb.dve, lambda: nc.vector.reciprocal(out=rz.t[:, 12:16], in_=rz.t[:, 12:16]), [rz], [rz])
                for br in range(3):
                    kb.op(kb.dve, lambda: nc.vector.tensor_tensor(out=fac.t[:, 4 * br:4 * br + 4], in0=rz.t[:, 4 * br:4 * br + 4],
                                                                  in1=gt[b].t[:, br * 8 + g * 4:br * 8 + g * 4 + 4], op=ALU.mult),
                          [rz, gt[b]], [fac])
                for h in range(4):
                    a = acc[h % 2]
                    kb.op(kb.dve, lambda: nc.vector.tensor_scalar(out=a.t[:, :], in0=pOC.t[:, h * 65:h * 65 + 64], scalar1=fac.t[:, h:h + 1],
                                                                  scalar2=None, op0=ALU.mult), [pOC, fac], [a])
                    kb.op(kb.dve, lambda: nc.vector.scalar_tensor_tensor(out=a.t[:, :], in0=pOS.t[:, h * 65:h * 65 + 64],
                                                                         scalar=fac.t[:, 4 + h:5 + h], in1=a.t[:, :], op0=ALU.mult, op1=ALU.add),
                          [pOS, fac, a], [a])
                    kb.op(kb.dve, lambda: nc.vector.scalar_tensor_tensor(out=ob.t[:, g * 256 + h * 64:g * 256 + (h + 1) * 64],
                                                                         in0=pOW.t[:, h * 65:h * 65 + 64], scalar=fac.t[:, 8 + h:9 + h],
                                                                         in1=a.t[:, :], op0=ALU.mult, op1=ALU.add), [pOW, fac, a], [ob])
                    kb.op(kb.dve, lambda: nc.vector.tensor_scalar(out=ob.t[:, 512 + g * 256 + h * 64:512 + g * 256 + (h + 1) * 64],
                                                                  in0=pOB.t[:, h * 65:h * 65 + 64], scalar1=rz.t[:, 12 + h:13 + h],
                                                                  scalar2=None, op0=ALU.mult), [pOB, rz], [ob])
            kb.dma(kb.sp, sc.o_d.t[ct * 128:(ct + 1) * 128, :], ob.t[:, :], reads=[ob], writes=[sc.o_d])

def merge_phase(kb, c, inp, sc, x1_d, x3_d, ntiles=None):
    nc = kb.nc
    TT = 256
    ntt = S // TT if ntiles is None else ntiles
    w_in = inp["w_in"]
    kb.barrier()
    with ExitStack() as es:
        kxT = kb.sb(es, "kxT", [128, 8, MEM], BF16)
        vx = kb.sb(es, "vx", [128, 2, D], BF16)
        gm = kb.sb(es, "g_mix2", [128, D], F32)
        gx = kb.sb(es, "g_x", [128, D], F32)
        ones = kb.sb(es, "ones", [128, 128], BF16)
        junk = kb.sb(es, "mjunk", [128, D], F32)
        ss = [kb.sb(es, "mss%d" % i, [128, 4], F32) for i in range(2)]
        hbf = [kb.sb(es, "mhbf%d" % i, [128, D], BF16) for i in range(2)]
        kb.op(kb.dve, lambda: nc.vector.memset(ones.t[:, :], 1.0), [], [ones])
        load_gain(kb, kb.sp, gm, inp["norm_mix"])
        load_gain(kb, kb.sp, gx, inp["norm_xattn"])
        with ExitStack() as es2:
            wkv = kb.sb(es2, "wkv", [128, 8, 2 * D], BF16)
            gmem = kb.sb(es2, "g_mem", [128, D], F32)
            ms = kb.sb(es2, "ms", [128, 2, D], F32)
            mTt = kb.sb(es2, "memT", [128, 8, MEM], BF16)
            load_gain(kb, kb.sp, gmem, inp["norm_mem"])
            kb.dma(kb.sp, ms.t[:, :, :], inp["mem"].rearrange("(s p) d -> p s d", p=128), writes=[ms])
            load_w(kb, kb.pool, wkv, inp["w_xkv"], 8)
            for s in range(2):
                rmsnorm_T(kb, c, ms.t[:, s, :], ms, gmem, mTt, slice(s * 128, (s + 1) * 128), junk, ss[s % 2], hbf[s % 2], c.ptr)
            for ch in range(8):
                pp = c.psb[ch % 2]
                for k in range(8):
                    kb.mm(pp.t[:, 0:MEM], wkv.t[:, k, ch * 128:(ch + 1) * 128], mTt.t[:, k, :], k == 0, k == 7, [wkv, mTt], [pp])
                kb.actv(kxT.t[:, ch, :], pp.t[:, 0:MEM], AF.Copy, [pp], [kxT])
            for mc in range(2):
                for o in range(2):
                    pp = c.psb[2 + o]
                    for k in range(8):
                        kb.mm(pp.t[:, :], mTt.t[:, k, mc * 128:(mc + 1) * 128], wkv.t[:, k, D + o * 512:D + (o + 1) * 512], k == 0, k == 7,
                              [wkv, mTt], [pp])
                    kb.actv(vx.t[:, mc, o * 512:(o + 1) * 512], pp.t[:, :], AF.Copy, [pp], [vx])
        kb.barrier()
        wga = kb.sb(es, "wga", [128, 8, 2048], BF16)
        wua = kb.sb(es, "wua", [128, 4, D], BF16)
        wub = kb.sb(es, "wub", [128, 4, D], BF16)
        wo = kb.sb(es, "wo", [128, 8, D], BF16)
        wxq = kb.sb(es, "wxq", [128, 8, D], BF16)
        wxo = kb.sb(es, "wxo", [128, 8, D], BF16)
        xs = [kb.sb(es, "mxs%d" % i, [128, 2, D], F32) for i in range(2)]
        os_ = [kb.sb(es, "mos%d" % i, [128, 2, D], BF16) for i in range(2)]
        x2 = kb.sb(es, "mx2", [128, 2, D], F32)
        x3 = [kb.sb(es, "mx3%d" % i, [128, D], F32) for i in range(2)]
        hT = kb.sb(es, "mhT", [128, 8, TT], BF16)
        oT = kb.sb(es, "moT", [128, 8, TT], BF16)
        mT = kb.sb(es, "mmT", [128, 8, TT], BF16)
        qxT = kb.sb(es, "mqxT", [128, 8, TT], BF16)
        oxT = kb.sb(es, "moxT", [128, 8, TT], BF16)
        ET = [kb.sb(es, "mET%d" % i, [128, 2, TT], BF16) for i in range(2)]
        sg = [kb.sb(es, "msg%d" % i, [128, 512], F32) for i in range(2)]
        mtmp = [kb.sb(es, "mtmp%d" % i, [128, 512], F32) for i in range(2)]
        rzx = [kb.sb(es, "mrz%d" % i, [128, TT], F32) for i in range(2)]
        load_w(kb, kb.pool, wga, w_in, 8, ncols=2048, col0=O_GA)
        load_w(kb, kb.pool, wua, inp["w_up_a"], 4)
        load_w(kb, kb.pool, wub, inp["w_up_b"], 4)
        load_w(kb, kb.pool, wo, inp["w_out"], 8)
        load_w(kb, kb.pool, wxq, inp["w_xq"], 8)
        load_w(kb, kb.pool, wxo, inp["w_xo"], 8)
        xview = x1_d.t.rearrange("(n s p) d -> n p s d", p=128, s=2)
        oview = sc.o_d.t.rearrange("(n s p) d -> n p s d", p=128, s=2)
        x3view = x3_d.t.rearrange("(n s p) d -> n p s d", p=128, s=2)

        def load(i):
            kb.dma(kb.sp, xs[i % 2].t[:, :, :], xview[i], reads=[x1_d], writes=[xs[i % 2]])
            kb.dma(kb.sp, os_[i % 2].t[:, :, :], oview[i], reads=[sc.o_d], writes=[os_[i % 2]])
        load(0)
        pA = c.psb[0:2]
        pB = c.psb[2:4]
        py = c.psb[4:6]
        na = 0
        for i in range(ntt):
            if i + 1 < ntt:
                load(i + 1)
            xt = xs[i % 2]
            ot = os_[i % 2]
            for s in range(2):
                rmsnorm_T(kb, c, xt.t[:, s, :], xt, gm, hT, slice(s * 128, (s + 1) * 128), junk, ss[s % 2], hbf[s % 2], c.ptr)
            for s in range(2):
                for k in range(8):
                    kb.tr(c.ptr.t[:, k, :], ot.t[:, s, k * 128:(k + 1) * 128], c.ident.t[:, :], [ot, c.ident], [c.ptr], inc=(k == 7))
                kb.op(kb.act, lambda: nc.scalar.copy(out=oT.t[:, 0:8, s * 128:(s + 1) * 128], in_=c.ptr.t[:, 0:8, :]), [c.ptr], [oT])
            for fc in range(8):
                a = pA[na % 2]
                bb = pB[na % 2]
                fs = slice(fc * 128, (fc + 1) * 128)
                for k in range(8):
                    kb.mm(a.t[:, 0:TT], wga.t[:, k, fc * 128:(fc + 1) * 128], hT.t[:, k, :], k == 0, k == 7, [wga, hT], [a])
                for k in range(8):
                    kb.mm(a.t[:, TT:2 * TT], wga.t[:, k, D + fc * 128:D + (fc + 1) * 128], hT.t[:, k, :], k == 0, k == 7, [wga, hT], [a])
                for k in range(4):
                    kb.mm(bb.t[:, 0:TT], wua.t[:, k, fs], oT.t[:, k, :], k == 0, k == 3, [wua, oT], [bb])
                for k in range(4):
                    kb.mm(bb.t[:, TT:2 * TT], wub.t[:, k, fs], oT.t[:, 4 + k, :], k == 0, k == 3, [wub, oT], [bb])
                sgj = sg[na % 2]
                tj = mtmp[na % 2]
                na += 1
                kb.actv(sgj.t[:, :], a.t[:, :], AF.Sigmoid, [a], [sgj])
                kb.op(kb.dve, lambda: nc.vector.tensor_tensor(out=tj.t[:, :], in0=sgj.t[:, :], in1=bb.t[:, :], op=ALU.mult), [sgj, bb], [tj])
                kb.op(kb.dve, lambda: nc.vector.tensor_tensor(out=mT.t[:, fc, :], in0=tj.t[:, 0:TT], in1=tj.t[:, TT:2 * TT], op=ALU.add),
                      [tj], [mT])
            for s in range(2):
                for o in range(2):
                    for fc in range(8):
                        kb.mm(py[o].t[:, :], mT.t[:, fc, s * 128:(s + 1) * 128], wo.t[:, fc, o * 512:(o + 1) * 512], fc == 0, fc == 7,
                              [mT, wo], [py[o]])
                for o in range(2):
                    kb.op(kb.dve, lambda: nc.vector.tensor_tensor(out=x2.t[:, s, o * 512:(o + 1) * 512], in0=py[o].t[:, :],
                                                                  in1=xt.t[:, s, o * 512:(o + 1) * 512], op=ALU.add), [py[o], xt], [x2])
            for s in range(2):
                rmsnorm_T(kb, c, x2.t[:, s, :], x2, gx, hT, slice(s * 128, (s + 1) * 128), junk, ss[s % 2], hbf[s % 2], c.ptr)
            for ch in range(8):
                a = pA[na % 2]
                na += 1
                for k in range(8):
                    kb.mm(a.t[:, 0:TT], wxq.t[:, k, ch * 128:(ch + 1) * 128], hT.t[:, k, :], k == 0, k == 7, [wxq, hT], [a])
                kb.actv(qxT.t[:, ch, :], a.t[:, 0:TT], AF.Copy, [a], [qxT], scale=1.0 / 16.0)
            for h in range(4):
                a = pA[na % 2]
                bb = pB[na % 2]
                rzz = rzx[na % 2]
                et = ET[na % 2]
                na += 1
                for mc in range(2):
                    for dc in range(2):
                        kb.mm(a.t[:, mc * TT:(mc + 1) * TT], kxT.t[:, h * 2 + dc, mc * 128:(mc + 1) * 128], qxT.t[:, h * 2 + dc, :],
                              dc == 0, dc == 1, [kxT, qxT], [a])
                kb.actv(et.t[:, :, :], a.t[:, :].rearrange("p (m q) -> p m q", m=2), AF.Exp, [a], [et])
                a2 = pA[na % 2]
                na += 1
                for dc in range(2):
                    for mc in range(2):
                        kb.mm(a2.t[:, dc * TT:(dc + 1) * TT], vx.t[:, mc, (h * 2 + dc) * 128:(h * 2 + dc + 1) * 128], et.t[:, mc, :],
                              mc == 0, mc == 1, [vx, et], [a2])
                for mc in range(2):
                    kb.mm(bb.t[:, 0:TT], ones.t[:, :], et.t[:, mc, :], mc == 0, mc == 1, [ones, et], [bb])
                kb.op(kb.dve, lambda: nc.vector.reciprocal(out=rzz.t[:, :], in_=bb.t[:, 0:TT]), [bb], [rzz])
                for dc in range(2):
                    kb.op(kb.dve, lambda: nc.vector.tensor_tensor(out=oxT.t[:, h * 2 + dc, :], in0=a2.t[:, dc * TT:(dc + 1) * TT],
                                                                  in1=rzz.t[:, :], op=ALU.mult), [a2, rzz], [oxT])
            for s in range(2):
                for o in range(2):
                    for ch in range(8):
                        kb.mm(py[o].t[:, :], oxT.t[:, ch, s * 128:(s + 1) * 128], wxo.t[:, ch, o * 512:(o + 1) * 512], ch == 0, ch == 7,
                              [oxT, wxo], [py[o]])
                xo = x3[s % 2]
                for o in range(2):
                    kb.op(kb.dve, lambda: nc.vector.tensor_tensor(out=xo.t[:, o * 512:(o + 1) * 512], in0=py[o].t[:, :],
                                                                  in1=x2.t[:, s, o * 512:(o + 1) * 512], op=ALU.add), [py[o], x2], [xo])
                kb.dma(kb.sp, x3view[i][:, s, :], xo.t[:, :], reads=[xo], writes=[x3_d])


def build(stop_after=None, ntiles=None, taps=()):
    nc = bass.Bass("TRN2", target_bir_lowering=False)
    inp = {}

    def din(name, shape, dt=F32):
        inp[name] = nc.dram_tensor(name, list(shape), dt, kind="ExternalInput").ap()
        return inp[name]
    x = din("x", [S, D])
    din("mem", [MEM, D])
    for n in ("norm_ffn1", "norm_mix", "norm_xattn", "norm_mem", "norm_ffn2", "norm_final"):
        din(n, [D])
    for p in ("w1", "w2"):
        din(p + "_gate", [D, DFF]); din(p + "_up", [D, DFF]); din(p + "_down", [DFF, D])
    din("w_in", [D, INW])
    for p in ("k", "v"):
        din("cmp_pe_" + p, [32, 64]); din("cmp_w1_" + p, [2048, 256]); din("cmp_w2_" + p, [256, 64])
    din("attn_sinks", [8])
    din("w_up_a", [512, D]); din("w_up_b", [512, D]); din("w_out", [D, D])
    din("w_xq", [D, D]); din("w_xkv", [D, 2 * D]); din("w_xo", [D, D])
    din("ident", [128, 128], BF16)
    din("emaster", [128, S], BF16)
    din("tw", [128, 8, 640]); din("tb", [128, 8, 256]); din("tc", [16, 3, 8, 128])
    din("addt", [128, 256]); din("ovf", [128, 4, 128], BF16); din("ovn", [16, 256], BF16); din("b31", [128, 16])
    out = nc.dram_tensor("out", [S, D], F32, kind="ExternalOutput").ap()
    with ExitStack() as es:
        kb = KB(nc, es)
        c = Ctx()
        c.psb = [kb.ps(es, "psb%d" % i, [128, 512], F32) for i in range(7)]
        c.ptr = kb.ps(es, "ptr", [128, 8, 128], BF16)
        c.ident = kb.sb(es, "ident_sb", [128, 128], BF16)
        kb.dma(kb.sp, c.ident.t[:, :], inp["ident"], writes=[c.ident])
        c.epsb = kb.sb(es, "epsb", [128, 1], F32)
        kb.op(kb.dve, lambda: nc.vector.memset(c.epsb.t[:, :], EPS), [], [c.epsb])
        sc = Ctx()
        sc.qa_d = kb.dram("qa_d", [NT, 128, 4, 128], BF16)
        sc.qb_d = kb.dram("qb_d", [NT, 128, 4, 128], BF16)
        for n in ("kcT_d", "vcT_d", "ksT_d", "kwT_d", "kbT_d"):
            setattr(sc, n, kb.dram(n, [128, S], BF16))
        for n in ("vs_d", "vw_d", "vb_d"):
            setattr(sc, n, kb.dram(n, [S, 130], BF16))
        sc.g_d = kb.dram("g_d", [S, 24], F32)
        sc.o_d = kb.dram("o_d", [S, D], BF16)
        x1_d = kb.dram("x1_d", [S, D], F32)
        x3_d = kb.dram("x3_d", [S, D], F32)
        xin = Res(x, "x")
        xout = Res(out, "out")
        dbg = {}

        def tap(name, res, shape, dt):
            if name in taps:
                d = nc.dram_tensor("tap_" + name, list(shape), dt, kind="ExternalOutput").ap()
                src = res.t
                if not hasattr(src, "tensor"):
                    src = src[tuple(slice(None) for _ in shape)]
                kb.dma(kb.sp, d, src, reads=[res])

        def done():
            kb.finish()
            kb.check_deadlock()
            print("instructions:", kb.n_ins)
            return nc
        ffn_phase(kb, c, xin, x1_d, inp["norm_ffn1"], inp["w1_gate"], inp["w1_up"], inp["w1_down"], ntiles=ntiles and ntiles.get("ffn1"))
        tap("x1", x1_d, [S, D], F32)
        if stop_after == "ffn1":
            return done()
        proj_phase(kb, c, x1_d, inp, sc, ntiles=ntiles and ntiles.get("proj"))
        for n, shp in (("qa_d", [NT, 128, 4, 128]), ("ksT_d", [128, S]), ("kwT_d", [128, S]), ("kcT_d", [128, S]), ("vs_d", [S, 130]),
                       ("vb_d", [S, 130])):
            tap(n, getattr(sc, n), shp, BF16)
        tap("g_d", sc.g_d, [S, 24], F32)
        if stop_after == "proj":
            return done()
        with ExitStack() as es_at:
            at = Ctx()
            at.kcc = kb.sb(es_at, "kcc", [128, 512], BF16)
            at.vcc = kb.sb(es_at, "vcc", [128, 4, 130], BF16)
            at.vcn = kb.sb(es_at, "vcn", [16, NT, 130], BF16)
            cmp_phase(kb, c, inp, sc, at)
            tap("kcc", at.kcc, [128, 512], BF16)
            tap("vcc", at.vcc, [128, 4, 130], BF16)
            tap("vcn", at.vcn, [16, NT, 130], BF16)
            if stop_after == "cmp":
                return done()
            attn_phase(kb, c, inp, sc, at, ntiles=ntiles and ntiles.get("attn"))
        tap("o_d", sc.o_d, [S, D], BF16)
        if stop_after == "attn":
            return done()
        merge_phase(kb, c, inp, sc, x1_d, x3_d, ntiles=ntiles and ntiles.get("merge"))
        tap("x3", x3_d, [S, D], F32)
        if stop_after == "merge":
            return done()
        ffn_phase(kb, c, x3_d, xout, inp["norm_ffn2"], inp["w2_gate"], inp["w2_up"], inp["w2_down"], final_g=inp["norm_final"],
                  ntiles=ntiles and ntiles.get("ffn2"))
        return done()


def _rel_bucket(dist):
    dist = np.maximum(dist, 0)
    d = np.maximum(dist, 1).astype(np.float32)
    large = 16 + (np.log(d / np.float32(16)) / np.float32(np.log(128 / 16)) * np.float32(16)).astype(np.int32)
    large = np.minimum(large, 31)
    return np.where(dist < 16, dist, large)


def host_consts(rel_bias):
    bf = ml_dtypes.bfloat16
    cst = {}
    cst["ident"] = np.eye(128, dtype=bf)
    cst["emaster"] = (np.arange(S)[None, :] // 64 == np.arange(128)[:, None]).astype(bf)
    k = np.arange(128)[:, None]
    v = np.arange(640)[None, :]
    dist = v - k
    gat = rel_bias[_rel_bucket(dist)]
    valid = (dist >= 0) & (dist < 512)
    cst["tw"] = np.ascontiguousarray(np.where(valid[:, :, None], gat[:, :, 0:8], np.float32(NEG)).transpose(0, 2, 1)).astype(np.float32)
    validb = (dist[:, 0:256] >= 0) & (dist[:, 0:256] < 128)
    cst["tb"] = np.ascontiguousarray(np.where(validb[:, :, None], gat[:, 0:256, 8:16], np.float32(NEG)).transpose(0, 2, 1)).astype(np.float32)
    i = np.arange(16)[:, None]
    q = np.arange(128)[None, :]
    tc = np.zeros((16, 3, 8, 128), np.float32)
    for var, off in enumerate((113, 97, -31)):
        dc = q - 16 * i + off
        gc = rel_bias[_rel_bucket(dc)][:, :, 0:8]
        tc[:, var] = np.where((dc >= 0)[:, :, None], gc, np.float32(NEG)).transpose(0, 2, 1)
    cst["tc"] = tc
    u = np.arange(256)[None, :] - 128
    qq = np.arange(128)[:, None]
    cur = (qq >= 64).astype(np.int64)
    forced = (u == cur) | (u == cur - 1)
    noncausal = u > cur
    cst["addt"] = np.where(forced, 1.0e4, np.where(noncausal, -1.0e4, 0.0)).astype(np.float32)
    n = np.arange(512)[:, None]
    j = np.arange(128)[None, :]
    ov = np.maximum(np.minimum(16 * n + 32, 64 * j + 64) - np.maximum(16 * n, 64 * j), 0) / 32.0
    cst["ovf"] = np.ascontiguousarray(ov.reshape(4, 128, 128).transpose(1, 0, 2)).astype(bf)
    a = 16 * np.arange(16)[:, None] - 144
    b = 64 * (np.arange(256)[None, :] - 128)
    cst["ovn"] = (np.maximum(np.minimum(a + 32, b + 64) - np.maximum(a, b), 0) / 32.0).astype(bf)
    cst["b31"] = np.ascontiguousarray(np.broadcast_to(rel_bias[31][None, :], (128, 16))).astype(np.float32)
    return cst


_NC_CACHE = {}


def make_in_maps(inputs):
    f = lambda a: np.ascontiguousarray(np.asarray(a, dtype=np.float32))
    rel_bias = f(inputs["rel_bias"])
    cst = host_consts(rel_bias)
    shared = dict(cst)
    for n in ("norm_ffn1", "norm_mix", "norm_xattn", "norm_mem", "norm_ffn2", "w1_gate", "w1_up", "w1_down", "w2_gate", "w2_up", "w2_down",
              "w_in", "cmp_pe_k", "cmp_w1_k", "cmp_w2_k", "cmp_pe_v", "cmp_w1_v", "cmp_w2_v", "attn_sinks", "w_up_a", "w_up_b", "w_out",
              "w_xq", "w_xkv", "w_xo"):
        shared[n] = f(inputs[n])[0]
    shared["norm_final"] = f(inputs["norm_final"])
    x = f(inputs["x"])
    mem = f(inputs["mem"])
    maps = []
    for b in range(8):
        m = dict(shared)
        m["x"] = x[b]
        m["mem"] = mem[b]
        maps.append(m)
    return maps


def kernel(**inputs):
    if "nc" not in _NC_CACHE:
        _NC_CACHE["nc"] = build()
    nc = _NC_CACHE["nc"]
    maps = make_in_maps(inputs)
    res = run_bass_kernel_spmd(nc, maps, core_ids=list(range(8)))
    return np.stack([np.asarray(r["out"], dtype=np.float32) for r in res.results], axis=0)
```
